# Optimizing a Trainium2 kernel written in Bass

```python
import math
import jax, jax.numpy as jnp
from jax import lax
import numpy as np

D_MODEL = 1024
BATCH = 2
SEQ = 8192
DEPTH = 4

N_MIXERS = 4
ROPE_THETA = 500000.0
Q_BLOCK = 128
LN_EPS = 1e-5
RMS_EPS = 1e-6
MAX_POS_OFFSET = 4096
DIFF_HEADS = 8
DIFF_HEAD_DIM = 64
FOX_HEADS = 16
FOX_HEAD_DIM = 64
MLA_HEADS = 16
MLA_NOPE = 64
MLA_ROPE = 32
MLA_V = 64
MLA_Q_RANK = 384
MLA_KV_RANK = 256
MOBA_HEADS = 16
MOBA_HEAD_DIM = 64
MOBA_BLOCK = 256
MOBA_TOPK = 3
MOBA_Q_CHUNK = 32
PARTIAL_ROT = 64 // 4
D_FF = 2816
CONV_WIDTH = 3
DEEPNORM_ALPHA = (2 * DEPTH) ** 0.25
DEEPNORM_BETA = (8 * DEPTH) ** -0.25

kernel_name = 'hybrid_interleaved_diff_fox_mla_moba_convffn'


def _n_uses(m):
    return len(range(m, DEPTH, N_MIXERS))


def diff_lambda_init(layer):
    return 0.8 - 0.6 * math.exp(-0.3 * layer)


def layer_norm(x, g, b):
    xf = x.astype(jnp.float32)
    mu = xf.mean(-1, keepdims=True)
    var = jnp.square(xf - mu).mean(-1, keepdims=True)
    y = (xf - mu) * lax.rsqrt(var + LN_EPS) * g.astype(jnp.float32) + b.astype(jnp.float32)
    return y.astype(x.dtype)


def rms_norm(x, g):
    xf = x.astype(jnp.float32)
    y = xf * lax.rsqrt(jnp.mean(xf * xf, -1, keepdims=True) + RMS_EPS) * g.astype(jnp.float32)
    return y.astype(x.dtype)


def rotary_angles(positions, rot_dim):
    inv_freq = ROPE_THETA ** (-jnp.arange(0, rot_dim, 2, dtype=jnp.float32) / rot_dim)
    ang = positions.astype(jnp.float32)[..., None] * inv_freq
    return jnp.cos(ang), jnp.sin(ang)


def apply_rotary(x, cos, sin, rot_dim):
    expand = (1,) * (x.ndim - 3)
    cos = cos.reshape(cos.shape[:2] + expand + cos.shape[-1:])
    sin = sin.reshape(sin.shape[:2] + expand + sin.shape[-1:])
    half = rot_dim // 2
    xr = x[..., :rot_dim].astype(jnp.float32)
    x1, x2 = xr[..., :half], xr[..., half:]
    rot = jnp.concatenate([x1 * cos - x2 * sin, x2 * cos + x1 * sin], -1).astype(x.dtype)
    return jnp.concatenate([rot, x[..., rot_dim:]], -1)


def causal_mask(q_start, q_len, k_len):
    q_idx = q_start + jnp.arange(q_len)
    k_idx = jnp.arange(k_len)
    return k_idx[None, :] <= q_idx[:, None]


def masked_softmax(scores, mask):
    s = jnp.where(mask, scores.astype(jnp.float32), -jnp.inf)
    return jax.nn.softmax(s, axis=-1)


def sweep_query_blocks(block_fn, seq, block):
    starts = jnp.arange(seq // block) * block
    out = lax.map(block_fn, starts)
    n, b, blk, f = out.shape
    return out.transpose(1, 0, 2, 3).reshape(b, n * blk, f)


def diff_attention(x, cos, sin, w_qkv, lam_q1, lam_k1, lam_q2, lam_k2, subln_g, w_o, lambda_init):
    B, S, _ = x.shape
    H, d = DIFF_HEADS, DIFF_HEAD_DIM
    q, k, v = jnp.split(x @ w_qkv, 3, axis=-1)
    q = apply_rotary(q.reshape(B, S, H, 2, d), cos, sin, PARTIAL_ROT).transpose(0, 2, 3, 1, 4)
    k = apply_rotary(k.reshape(B, S, H, 2, d), cos, sin, PARTIAL_ROT).transpose(0, 2, 3, 1, 4)
    v = v.reshape(B, S, H, 2 * d).transpose(0, 2, 1, 3)
    f32 = jnp.float32
    lam = (jnp.exp(jnp.sum(lam_q1.astype(f32) * lam_k1.astype(f32)))
           - jnp.exp(jnp.sum(lam_q2.astype(f32) * lam_k2.astype(f32))) + lambda_init)
    scale = d ** -0.5

    def block(start):
        qb = lax.dynamic_slice_in_dim(q, start, Q_BLOCK, axis=3)
        s = jnp.einsum('bhcqd,bhckd->bhcqk', qb, k) * scale
        p = masked_softmax(s, causal_mask(start, Q_BLOCK, S))
        a = (p[:, :, 0] - lam * p[:, :, 1]).astype(v.dtype)
        o = jnp.einsum('bhqk,bhkd->bqhd', a, v)
        o = rms_norm(o, subln_g) * (1.0 - lambda_init)
        return o.reshape(B, Q_BLOCK, H * 2 * d)

    return sweep_query_blocks(block, S, Q_BLOCK) @ w_o


def forgetting_attention(x, w_in, b_f, w_o):
    B, S, _ = x.shape
    H, d = FOX_HEADS, FOX_HEAD_DIM
    q, k, v, f_logit = jnp.split(x @ w_in, [H * d, 2 * H * d, 3 * H * d], axis=-1)
    q = q.reshape(B, S, H, d).transpose(0, 2, 1, 3)
    k = k.reshape(B, S, H, d).transpose(0, 2, 1, 3)
    v = v.reshape(B, S, H, d).transpose(0, 2, 1, 3)
    log_f = jax.nn.log_sigmoid((f_logit + b_f).astype(jnp.float32))
    c = jnp.cumsum(log_f, axis=1).transpose(0, 2, 1)
    scale = d ** -0.5

    def block(start):
        qb = lax.dynamic_slice_in_dim(q, start, Q_BLOCK, axis=2)
        cb = lax.dynamic_slice_in_dim(c, start, Q_BLOCK, axis=2)
        s = (jnp.einsum('bhqd,bhkd->bhqk', qb, k).astype(jnp.float32) * scale
             + cb[..., :, None] - c[..., None, :])
        p = masked_softmax(s, causal_mask(start, Q_BLOCK, S)).astype(v.dtype)
        o = jnp.einsum('bhqk,bhkd->bqhd', p, v)
        return o.reshape(B, Q_BLOCK, H * d)

    return sweep_query_blocks(block, S, Q_BLOCK) @ w_o


def latent_attention(x, cos, sin, w_down, q_norm_g, kv_norm_g, w_uq, w_ukv, w_o):
    B, S, _ = x.shape
    H = MLA_HEADS
    c_q, c_kv, k_rope = jnp.split(x @ w_down, [MLA_Q_RANK, MLA_Q_RANK + MLA_KV_RANK], axis=-1)
    c_q = rms_norm(c_q, q_norm_g)
    c_kv = rms_norm(c_kv, kv_norm_g)
    q = (c_q @ w_uq).reshape(B, S, H, MLA_NOPE + MLA_ROPE)
    q_nope = q[..., :MLA_NOPE].transpose(0, 2, 1, 3)
    q_rope = apply_rotary(q[..., MLA_NOPE:], cos, sin, MLA_ROPE).transpose(0, 2, 1, 3)
    kv = (c_kv @ w_ukv).reshape(B, S, H, MLA_NOPE + MLA_V)
    k_nope = kv[..., :MLA_NOPE].transpose(0, 2, 1, 3)
    v = kv[..., MLA_NOPE:].transpose(0, 2, 1, 3)
    k_rope = apply_rotary(k_rope, cos, sin, MLA_ROPE)
    scale = (MLA_NOPE + MLA_ROPE) ** -0.5

    def block(start):
        qn = lax.dynamic_slice_in_dim(q_nope, start, Q_BLOCK, axis=2)
        qr = lax.dynamic_slice_in_dim(q_rope, start, Q_BLOCK, axis=2)
        s = (jnp.einsum('bhqd,bhkd->bhqk', qn, k_nope)
             + jnp.einsum('bhqr,bkr->bhqk', qr, k_rope)) * scale
        p = masked_softmax(s, causal_mask(start, Q_BLOCK, S)).astype(v.dtype)
        o = jnp.einsum('bhqk,bhkd->bqhd', p, v)
        return o.reshape(B, Q_BLOCK, H * MLA_V)

    return sweep_query_blocks(block, S, Q_BLOCK) @ w_o


def moba_attention(x, cos, sin, w_qkv, w_o):
    B, S, _ = x.shape
    H, d, BS, QC = MOBA_HEADS, MOBA_HEAD_DIM, MOBA_BLOCK, MOBA_Q_CHUNK
    q, k, v = jnp.split(x @ w_qkv, 3, axis=-1)
    q = apply_rotary(q.reshape(B, S, H, d), cos, sin, PARTIAL_ROT).transpose(0, 2, 1, 3)
    k = apply_rotary(k.reshape(B, S, H, d), cos, sin, PARTIAL_ROT).transpose(0, 2, 1, 3)
    v = v.reshape(B, S, H, d).transpose(0, 2, 1, 3)
    nb = -(-S // BS)
    pad = nb * BS - S
    k_blocks = jnp.pad(k, ((0, 0), (0, 0), (0, pad), (0, 0))).reshape(B, H, nb, BS, d)
    v_blocks = jnp.pad(v, ((0, 0), (0, 0), (0, pad), (0, 0))).reshape(B, H, nb, BS, d)
    k_mean = k_blocks.astype(jnp.float32).mean(axis=3).astype(k.dtype)
    topk = min(MOBA_TOPK, max(nb - 1, 1))
    bidx = jnp.arange(B)[:, None, None, None]
    hidx = jnp.arange(H)[None, :, None, None]
    scale = d ** -0.5

    def chunk(start):
        qc = lax.dynamic_slice_in_dim(q, start, QC, axis=2)
        cur = start // BS
        gate = jnp.einsum('bhqd,bhnd->bhqn', qc, k_mean).astype(jnp.float32)
        gate = jnp.where(jnp.arange(nb) < cur, gate, -jnp.inf)
        _, idx = lax.top_k(gate, topk)
        sel_ok = idx < cur
        k_sel = k_blocks[bidx, hidx, idx]
        v_sel = v_blocks[bidx, hidx, idx]
        s_sel = jnp.einsum('bhqd,bhqnkd->bhqnk', qc, k_sel).reshape(B, H, QC, topk * BS)
        k_own = lax.dynamic_index_in_dim(k_blocks, cur, axis=2, keepdims=False)
        v_own = lax.dynamic_index_in_dim(v_blocks, cur, axis=2, keepdims=False)
        s_own = jnp.einsum('bhqd,bhkd->bhqk', qc, k_own)
        own_mask = (cur * BS + jnp.arange(BS))[None, :] <= (start + jnp.arange(QC))[:, None]
        s = jnp.concatenate([s_sel, s_own], -1) * scale
        mask = jnp.concatenate([jnp.repeat(sel_ok, BS, axis=-1),
                                jnp.broadcast_to(own_mask, (B, H, QC, BS))], -1)
        p = masked_softmax(s, mask).astype(v.dtype)
        p_sel = p[..., :topk * BS].reshape(B, H, QC, topk, BS)
        o = (jnp.einsum('bhqnk,bhqnkd->bqhd', p_sel, v_sel)
             + jnp.einsum('bhqk,bhkd->bqhd', p[..., topk * BS:], v_own))
        return o.reshape(B, QC, H * d)

    return sweep_query_blocks(chunk, S, QC) @ w_o


def conv_ffn(x, w_in, conv_w, conv_b, w_out):
    gate, up = jnp.split(x @ w_in, 2, axis=-1)
    gate = lax.conv_general_dilated(
        gate, conv_w[:, None, :], window_strides=(1,), padding=[(CONV_WIDTH - 1, 0)],
        dimension_numbers=('NWC', 'WIO', 'NWC'), feature_group_count=D_FF) + conv_b
    return (jax.nn.silu(gate) * up) @ w_out


def setup_inputs(seed: int = 0) -> dict:
    key = jax.random.key(seed)
    keys = iter(jax.random.split(key, 32))
    f32 = jnp.float32

    def normal(shape, std):
        return std * jax.random.normal(next(keys), shape, f32)

    def gain(shape):
        return 1.0 + normal(shape, 0.02)

    n_diff, n_fox, n_mla, n_moba = [_n_uses(m) for m in range(N_MIXERS)]
    D = D_MODEL
    beta = DEEPNORM_BETA
    x = normal((BATCH, SEQ, D), 1.0)
    offset = jax.random.randint(next(keys), (BATCH, 1), 0, MAX_POS_OFFSET, dtype=jnp.int32)
    positions = jnp.arange(SEQ, dtype=jnp.int32)[None, :] + offset
    diff_w = DIFF_HEADS * 2 * DIFF_HEAD_DIM
    fox_w = FOX_HEADS * FOX_HEAD_DIM
    moba_w = MOBA_HEADS * MOBA_HEAD_DIM
    return {
        'x': x,
        'positions': positions,
        'diff_w_qkv': normal((n_diff, D, 3 * diff_w), D ** -0.5),
        'diff_lambda_q1': normal((n_diff, DIFF_HEAD_DIM), 0.1),
        'diff_lambda_k1': normal((n_diff, DIFF_HEAD_DIM), 0.1),
        'diff_lambda_q2': normal((n_diff, DIFF_HEAD_DIM), 0.1),
        'diff_lambda_k2': normal((n_diff, DIFF_HEAD_DIM), 0.1),
        'diff_subln_g': gain((n_diff, 2 * DIFF_HEAD_DIM)),
        'diff_w_o': normal((n_diff, diff_w, D), beta * diff_w ** -0.5),
        'fox_w_in': normal((n_fox, D, 3 * fox_w + FOX_HEADS), D ** -0.5),
        'fox_b_f': jax.random.uniform(next(keys), (n_fox, FOX_HEADS), f32, 1.0, 5.0),
        'fox_w_o': normal((n_fox, fox_w, D), beta * fox_w ** -0.5),
        'mla_w_down': normal((n_mla, D, MLA_Q_RANK + MLA_KV_RANK + MLA_ROPE), D ** -0.5),
        'mla_q_norm_g': gain((n_mla, MLA_Q_RANK)),
        'mla_kv_norm_g': gain((n_mla, MLA_KV_RANK)),
        'mla_w_uq': normal((n_mla, MLA_Q_RANK, MLA_HEADS * (MLA_NOPE + MLA_ROPE)), MLA_Q_RANK ** -0.5),
        'mla_w_ukv': normal((n_mla, MLA_KV_RANK, MLA_HEADS * (MLA_NOPE + MLA_V)), MLA_KV_RANK ** -0.5),
        'mla_w_o': normal((n_mla, MLA_HEADS * MLA_V, D), beta * (MLA_HEADS * MLA_V) ** -0.5),
        'moba_w_qkv': normal((n_moba, D, 3 * moba_w), D ** -0.5),
        'moba_w_o': normal((n_moba, moba_w, D), beta * moba_w ** -0.5),
        'ffn_w_in': normal((DEPTH, D, 2 * D_FF), D ** -0.5),
        'ffn_conv_w': normal((DEPTH, CONV_WIDTH, D_FF), CONV_WIDTH ** -0.5),
        'ffn_conv_b': normal((DEPTH, D_FF), 0.02),
        'ffn_w_out': normal((DEPTH, D_FF, D), beta * D_FF ** -0.5),
        'ln1_g': gain((DEPTH, D)),
        'ln1_b': normal((DEPTH, D), 0.02),
        'ln2_g': gain((DEPTH, D)),
        'ln2_b': normal((DEPTH, D), 0.02),
    }


def reference(x, positions, diff_w_qkv, diff_lambda_q1, diff_lambda_k1, diff_lambda_q2, diff_lambda_k2,
              diff_subln_g, diff_w_o, fox_w_in, fox_b_f, fox_w_o, mla_w_down, mla_q_norm_g, mla_kv_norm_g,
              mla_w_uq, mla_w_ukv, mla_w_o, moba_w_qkv, moba_w_o, ffn_w_in, ffn_conv_w, ffn_conv_b,
              ffn_w_out, ln1_g, ln1_b, ln2_g, ln2_b):
    cos_p, sin_p = rotary_angles(positions, PARTIAL_ROT)
    cos_m, sin_m = rotary_angles(positions, MLA_ROPE)
    h = x
    for i in range(DEPTH):
        m, u = i % N_MIXERS, i // N_MIXERS
        if m == 0:
            y = diff_attention(h, cos_p, sin_p, diff_w_qkv[u], diff_lambda_q1[u], diff_lambda_k1[u],
                               diff_lambda_q2[u], diff_lambda_k2[u], diff_subln_g[u], diff_w_o[u],
                               diff_lambda_init(i))
        elif m == 1:
            y = forgetting_attention(h, fox_w_in[u], fox_b_f[u], fox_w_o[u])
        elif m == 2:
            y = latent_attention(h, cos_m, sin_m, mla_w_down[u], mla_q_norm_g[u], mla_kv_norm_g[u],
                                 mla_w_uq[u], mla_w_ukv[u], mla_w_o[u])
        else:
            y = moba_attention(h, cos_p, sin_p, moba_w_qkv[u], moba_w_o[u])
        h = layer_norm(DEEPNORM_ALPHA * h + y, ln1_g[i], ln1_b[i])
        f = conv_ffn(h, ffn_w_in[i], ffn_conv_w[i], ffn_conv_b[i], ffn_w_out[i])
        h = layer_norm(DEEPNORM_ALPHA * h + f, ln2_g[i], ln2_b[i])
    return h
```

```python
import math, os
import numpy as np
import ml_dtypes
from concourse.bass_utils import run_bass_kernel_spmd
bf16 = ml_dtypes.bfloat16

import numpy as np
import concourse.bass as bass
import concourse.mybir as mybir
from contextlib import ExitStack

F32 = mybir.dt.float32
BF16 = mybir.dt.bfloat16
I32 = mybir.dt.int32
AF = mybir.ActivationFunctionType
ALU = mybir.AluOpType
AX = mybir.AxisListType

ENGS = ("sync", "scalar", "vector", "gpsimd", "tensor")


class Sem:
    def __init__(self, h, step):
        self.h = h
        self.step = step
        self.count = 0


class Prog:
    def __init__(self):
        self.nc = bass.Bass("TRN2", target_bir_lowering=False)
        self.es = ExitStack()
        self.q = {e: [] for e in ENGS}
        self.esem = {}
        self.dsem = {}
        self.lastw = {}
        self.readers = {}
        self.last_tok = None

    def dram_in(self, name, shape, dt):
        return self.nc.dram_tensor(name, list(shape), dt, kind="ExternalInput").ap()

    def dram_out(self, name, shape, dt):
        return self.nc.dram_tensor(name, list(shape), dt, kind="ExternalOutput").ap()

    def dram_tmp(self, name, shape, dt):
        return self.nc.dram_tensor(name, list(shape), dt).ap()

    def sb(self, name, shape, dt):
        return self.es.enter_context(self.nc.sbuf_tensor(name, list(shape), dt))

    def ps(self, name, shape, dt=F32):
        return self.es.enter_context(self.nc.psum_tensor(name, list(shape), dt))

    def _sem(self, name, step):
        h = self.es.enter_context(self.nc.semaphore(name))
        return Sem(h, step)

    def op(self, eng, f, reads=(), writes=(), dma=None, after=()):
        deps = [t for t in after if t is not None]
        for r in reads:
            if r in self.lastw:
                deps.append(self.lastw[r])
        for w in writes:
            if w in self.lastw:
                deps.append(self.lastw[w])
            deps.extend(self.readers.get(w, ()))
        if dma is not None:
            if dma not in self.dsem:
                nm = "d_" + "_".join(str(x) for x in (dma if isinstance(dma, tuple) else (dma,)))
                self.dsem[dma] = self._sem(nm, 16)
            s = self.dsem[dma]
        else:
            if eng not in self.esem:
                self.esem[eng] = self._sem("e_" + eng, 1)
            s = self.esem[eng]
        s.count += s.step
        tok = (s, s.count, eng if dma is None else "dma")
        best = {}
        for (ds, dv, de) in deps:
            if de == "tensor" and eng == "tensor" and dma is None:
                continue
            if best.get(id(ds), (None, 0))[1] < dv:
                best[id(ds)] = (ds, dv)
        self.q[eng].append((f, list(best.values()), s))
        for r in reads:
            self.readers.setdefault(r, []).append(tok)
        for w in writes:
            self.lastw[w] = tok
            self.readers[w] = []
        self.last_tok = tok
        return tok

    def wait(self, eng, toks):
        self.q[eng].append((None, [(t[0], t[1]) for t in toks], None))

    def emit(self):
        nc = self.nc
        with nc.Block() as block:
            for e in ENGS:
                ops = self.q[e]
                if not ops:
                    continue

                def body(eng, ops=ops, e=e):
                    seen = {}
                    for (f, waits, sig) in ops:
                        for (s, x) in waits:
                            if seen.get(id(s), 0) >= x:
                                continue
                            seen[id(s)] = x
                            eng.wait_ge(s.h, x)
                        if f is None:
                            continue
                        ins = f(eng)
                        ins.then_inc(sig.h, sig.step)

                getattr(block, e)(body)
        self.es.close()
        return nc


ALPHA = (2 * 4) ** 0.25
LN_EPS = 1e-5
NT = 17
TT = NT * 128
NFC = 22


def build_cast(ncols, chunk=2048):
    P = Prog()
    x = P.dram_in("x", [128, ncols], F32)
    y = P.dram_out("y", [128, ncols], BF16)
    nch = (ncols + chunk - 1) // chunk
    xs = [P.sb(f"xs{i}", [128, chunk], F32) for i in range(3)]
    ys = [P.sb(f"ys{i}", [128, chunk], BF16) for i in range(3)]
    for i in range(nch):
        c0 = i * chunk; c1 = min(ncols, c0 + chunk); n = c1 - c0; b = i % 3
        P.op("sync", lambda e, b=b, c0=c0, c1=c1, n=n: e.dma_start(out=xs[b][:, 0:n], in_=x[:, c0:c1]),
             writes=[("xs", b)], dma=("ld", b))
        eng = "vector" if i % 2 == 0 else "gpsimd"
        P.op(eng, lambda e, b=b, n=n: e.tensor_copy(out=ys[b][:, 0:n], in_=xs[b][:, 0:n]), reads=[("xs", b)], writes=[("ys", b)])
        t = P.op("scalar", lambda e, b=b, c0=c0, c1=c1, n=n: e.dma_start(out=y[:, c0:c1], in_=ys[b][:, 0:n]),
                 reads=[("ys", b)], dma=("st", b))
    P.wait("scalar", [(sm, sm.count, "dma") for kk, sm in P.dsem.items() if kk[0] == "st"])
    return P.emit()


def build_F():
    P = Prog()
    oT = P.dram_in("oT", [8, 128, TT], BF16)
    h = P.dram_in("h", [TT, 1024], F32)
    wo = P.dram_in("wo", [8, 128, 1024], BF16)
    lnp = P.dram_in("lnp", [4, 1024], F32)
    win = P.dram_in("win", [2 * NFC, 128, 8, 128], BF16)
    cw = P.dram_in("cw", [128, NFC, 4], F32)
    wout = P.dram_in("wout", [NFC, 128, 1024], BF16)
    flag = P.dram_in("flag", [128, 1], F32)
    h1s = P.dram_tmp("h1s", [NT, 128, 1024], F32)
    ho = P.dram_out("ho", [16, 128, 1024], F32)
    hbo = P.dram_out("hbo", [16, 128, 1024], BF16)

    wo_sb = P.sb("wo_sb", [128, 8, 1024], BF16)
    wout_sb = P.sb("wout_sb", [128, NFC, 1024], BF16)
    ln_sb = P.sb("ln_sb", [128, 4, 1024], F32)
    cw_sb = P.sb("cw_sb", [128, NFC, 4], F32)
    flag_sb = P.sb("flag_sb", [128, 1], F32)
    eps_sb = P.sb("eps_sb", [128, 1], F32)
    ident = P.sb("ident", [128, 128], BF16)
    identf = P.sb("identf", [128, 128], F32)
    h1T = P.sb("h1T", [128, 8, 1024 + 128], BF16)
    actT = P.sb("actT", [128, NFC, 1024], BF16)
    gate_sb = P.sb("gate_sb", [128, 2 + 1024], F32)
    up_sb = [P.sb(f"up_sb{i}", [128, 512], F32) for i in range(2)]
    cv_sb = [P.sb(f"cv_sb{i}", [128, 512], F32) for i in range(2)]
    sl_sb = [P.sb(f"sl_sb{i}", [128, 512], F32) for i in range(2)]
    wg_sb = [P.sb(f"wg_sb{i}", [128, 8, 128], BF16) for i in range(2)]
    wu_sb = [P.sb(f"wu_sb{i}", [128, 8, 128], BF16) for i in range(2)]
    oT_sb = [P.sb(f"oT_sb{i}", [128, 8, 128], BF16) for i in range(2)]
    h_sb = [P.sb(f"h_sb{i}", [128, 1024], F32) for i in range(2)]
    z_sb = [P.sb(f"z_sb{i}", [128, 1024], F32) for i in range(2)]
    hb_sb = [P.sb(f"hb_sb{i}", [128, 1024], BF16) for i in range(2)]
    st_sb = [P.sb(f"st_sb{i}", [128, 16], F32) for i in range(2)]
    psA = [P.ps(f"psA{i}", [128, 1024], F32) for i in range(2)]
    psG = [P.ps(f"psG{i}", [128, 512], F32) for i in range(2)]
    psU = P.ps("psU", [128, 512], F32)
    psT = P.ps("psT", [128, 8, 128], BF16)

    P.op("sync", lambda e: e.dma_start(out=wo_sb[:], in_=wo.rearrange("c p d -> p c d")), writes=["wo"], dma="l_wo")
    P.op("sync", lambda e: e.dma_start(out=ln_sb[:], in_=lnp.partition_broadcast(128)), writes=["ln"], dma="l_ln")
    P.op("sync", lambda e: e.dma_start(out=cw_sb[:], in_=cw), writes=["cw"], dma="l_cw")
    P.op("sync", lambda e: e.dma_start(out=flag_sb[:], in_=flag), writes=["flag"], dma="l_flag")
    P.op("gpsimd", lambda e: e.memset(eps_sb[:], LN_EPS), writes=["eps"])
    P.op("gpsimd", lambda e: e.memset(identf[:], 1.0), writes=["identf"])
    P.op("gpsimd", lambda e: e.affine_select(out=identf[:], in_=identf[:], pattern=[[1, 128]],
                                             compare_op=ALU.is_equal, fill=0.0, base=0, channel_multiplier=-1),
         reads=["identf"], writes=["identf"])
    P.op("gpsimd", lambda e: e.tensor_copy(out=ident[:], in_=identf[:]), reads=["identf"], writes=["ident"])
    P.op("gpsimd", lambda e: e.dma_start(out=wout_sb[:, 0:11, :], in_=wout[0:11].rearrange("c p d -> p c d")), writes=["wout0"], dma="l_wout0")
    P.op("gpsimd", lambda e: e.dma_start(out=wout_sb[:, 11:22, :], in_=wout[11:22].rearrange("c p d -> p c d")), writes=["wout1"], dma="l_wout1")

    def layer_norm_tile(b, gi, bi):
        z = z_sb[b]; stt = st_sb[b]; Z = ("z", b); S = ("st", b)
        P.op("vector", lambda e: e.bn_stats(out=stt[:, 0:6], in_=z[:, 0:512]), reads=[Z], writes=[(S, 0)])
        P.op("vector", lambda e: e.bn_stats(out=stt[:, 6:12], in_=z[:, 512:1024]), reads=[Z], writes=[(S, 1)])
        P.op("vector", lambda e: e.bn_aggr(out=stt[:, 12:14], in_=stt[:, 0:12]), reads=[(S, 0), (S, 1)], writes=[(S, 2)])
        P.op("scalar", lambda e: e.activation(out=stt[:, 14:15], in_=stt[:, 13:14], func=AF.Sqrt, bias=eps_sb[:, 0:1], scale=1.0),
             reads=[(S, 2), "eps"], writes=[(S, 3)])
        P.op("vector", lambda e: e.reciprocal(out=stt[:, 15:16], in_=stt[:, 14:15]), reads=[(S, 3)], writes=[(S, 4)])
        P.op("vector", lambda e: e.tensor_scalar(out=z[:], in0=z[:], scalar1=stt[:, 12:13], scalar2=stt[:, 15:16],
                                                 op0=ALU.subtract, op1=ALU.mult), reads=[Z, (S, 2), (S, 4)], writes=[Z])
        P.op("gpsimd", lambda e: e.tensor_tensor(out=z[:], in0=z[:], in1=ln_sb[:, gi, :], op=ALU.mult), reads=[Z, "ln"], writes=[Z])
        P.op("gpsimd", lambda e: e.tensor_tensor(out=z[:], in0=z[:], in1=ln_sb[:, bi, :], op=ALU.add), reads=[Z, "ln"], writes=[Z])

    cnt = dict(k=0, q=0, bi=0, ui=0, kc=0)
    halves = [list(range(0, 9)), list(range(9, 17))]
    for hf, tiles in enumerate(halves):
        if hf == 1:
            P.op("vector", lambda e: e.tensor_copy(out=h1T[:, :, 0:128], in_=h1T[:, :, 1024:1152]),
                 reads=[("h1T", 8)], writes=[("h1T", 0)])
        for t in tiles:
            k = cnt["k"]; cnt["k"] += 1
            b = k % 2
            slot = (t - tiles[0]) if hf == 0 else (t - tiles[0] + 1)
            col0 = slot * 128
            P.op("sync", lambda e, b=b, t=t: e.dma_start(out=oT_sb[b][:], in_=oT[:, :, t * 128:(t + 1) * 128].rearrange("c p t -> p c t")),
                 writes=[("oT", b)], dma=("l_oT", b))
            P.op("sync", lambda e, b=b, t=t: e.dma_start(out=h_sb[b][:], in_=h[t * 128:(t + 1) * 128, :]),
                 writes=[("h", b)], dma=("l_h", b))
            for half in range(2):
                for c in range(8):
                    P.op("tensor", lambda e, b=b, c=c, half=half: e.matmul(psA[b][:, half * 512:(half + 1) * 512], lhsT=oT_sb[b][:, c, :],
                                                                           rhs=wo_sb[:, c, half * 512:(half + 1) * 512], start=(c == 0), stop=(c == 7)),
                         reads=[("oT", b), "wo"], writes=[("psA", b)])
            P.op("vector", lambda e, b=b: e.scalar_tensor_tensor(out=z_sb[b][:], in0=h_sb[b][:], scalar=ALPHA, in1=psA[b][:],
                                                                 op0=ALU.mult, op1=ALU.add),
                 reads=[("h", b), ("psA", b)], writes=[("z", b)])
            layer_norm_tile(b, 0, 1)
            P.op("gpsimd", lambda e, b=b, t=t: e.dma_start(out=h1s[t], in_=z_sb[b][:]), reads=[("z", b)], writes=[("h1s", t)], dma=("s_h1", b))
            P.op("scalar", lambda e, b=b: e.activation(out=hb_sb[b][:], in_=z_sb[b][:], func=AF.Copy), reads=[("z", b)], writes=[("hb", b)])
            for c in range(8):
                P.op("tensor", lambda e, b=b, c=c: e.transpose(psT[:, c, :], hb_sb[b][:, c * 128:(c + 1) * 128], ident[:]),
                     reads=[("hb", b), "ident"], writes=["psT"])
            P.op("vector", lambda e, col0=col0: e.tensor_copy(out=h1T[:, :, col0:col0 + 128], in_=psT[:]),
                 reads=["psT"], writes=[("h1T", slot)])

        for fc in range(NFC):
            q = cnt["q"]; cnt["q"] += 1
            wb = q % 2
            P.op("sync", lambda e, wb=wb, fc=fc: e.dma_start(out=wg_sb[wb][:], in_=win[fc]), writes=[("wg", wb)], dma=("l_wg", wb))
            P.op("sync", lambda e, wb=wb, fc=fc: e.dma_start(out=wu_sb[wb][:], in_=win[NFC + fc]), writes=[("wu", wb)], dma=("l_wu", wb))
            for blk in range(3):
                bi = cnt["bi"]; cnt["bi"] += 1
                gb = bi % 2
                if blk == 0:
                    slots = [0]; c0 = 0; n = 128
                else:
                    j = blk - 1
                    slots = [1 + 4 * j + i for i in range(4)]; c0 = 128 + j * 512; n = 512
                rd = [("h1T", s) for s in slots]
                for c in range(8):
                    P.op("tensor", lambda e, gb=gb, c=c, c0=c0, n=n, wb=wb: e.matmul(psG[gb][:, 0:n], lhsT=wg_sb[wb][:, c, :], rhs=h1T[:, c, c0:c0 + n],
                                                                                      start=(c == 0), stop=(c == 7)),
                         reads=rd + [("wg", wb)], writes=[("psG", gb)])
                if blk == 0:
                    P.op("vector", lambda e, gb=gb, hf=hf: e.tensor_scalar(out=gate_sb[:, 0:2], in0=psG[gb][:, 126:128], scalar1=(flag_sb[:, 0:1] if hf == 0 else 1.0), scalar2=None, op0=ALU.mult),
                         reads=[("psG", gb), "flag"], writes=[("gate", "halo")])
                    continue
                ui = cnt["ui"]; cnt["ui"] += 1
                ub = ui % 2
                for c in range(8):
                    P.op("tensor", lambda e, c=c, c0=c0, wb=wb: e.matmul(psU[:, :], lhsT=wu_sb[wb][:, c, :], rhs=h1T[:, c, c0:c0 + 512],
                                                                          start=(c == 0), stop=(c == 7)),
                         reads=rd + [("wu", wb)], writes=["psU"])
                base = 2 + j * 512
                P.op("scalar", lambda e, gb=gb, base=base: e.activation(out=gate_sb[:, base:base + 512], in_=psG[gb][:, :], func=AF.Copy),
                     reads=[("psG", gb)], writes=[("gate", j)])
                P.op("scalar", lambda e, ub=ub: e.activation(out=up_sb[ub][:], in_=psU[:, :], func=AF.Copy), reads=["psU"], writes=[("up", ub)])
                prev = ("gate", "halo") if j == 0 else ("gate", j - 1)
                P.op("vector", lambda e, ub=ub, base=base, fc=fc: e.tensor_scalar(out=cv_sb[ub][:], in0=gate_sb[:, base - 2: base + 510],
                                                                                   scalar1=cw_sb[:, fc, 0:1], scalar2=None, op0=ALU.mult),
                     reads=[("gate", j), prev, "cw"], writes=[("cv", ub)])
                P.op("vector", lambda e, ub=ub, base=base, fc=fc: e.scalar_tensor_tensor(out=cv_sb[ub][:], in0=gate_sb[:, base - 1: base + 511],
                                                                                          scalar=cw_sb[:, fc, 1:2], in1=cv_sb[ub][:], op0=ALU.mult, op1=ALU.add),
                     reads=[("gate", j), prev, ("cv", ub)], writes=[("cv", ub)])
                P.op("vector", lambda e, ub=ub, base=base, fc=fc: e.scalar_tensor_tensor(out=cv_sb[ub][:], in0=gate_sb[:, base: base + 512],
                                                                                          scalar=cw_sb[:, fc, 2:3], in1=cv_sb[ub][:], op0=ALU.mult, op1=ALU.add),
                     reads=[("gate", j), ("cv", ub)], writes=[("cv", ub)])
                P.op("scalar", lambda e, ub=ub, fc=fc: e.activation(out=sl_sb[ub][:], in_=cv_sb[ub][:], func=AF.Silu, bias=cw_sb[:, fc, 3:4], scale=1.0),
                     reads=[("cv", ub), "cw"], writes=[("sl", ub)])
                P.op("gpsimd", lambda e, ub=ub, fc=fc, j=j: e.tensor_tensor(out=actT[:, fc, j * 512:(j + 1) * 512], in0=sl_sb[ub][:], in1=up_sb[ub][:], op=ALU.mult),
                     reads=[("sl", ub), ("up", ub)], writes=[("actT", fc, j)])

        for t in [t for t in tiles if t >= 1]:
            kc = cnt["kc"]; cnt["kc"] += 1
            b = kc % 2
            tm = t - 1
            slot = (t - tiles[0]) if hf == 0 else (t - tiles[0] + 1)
            lcol = (slot - 1) * 128
            j = lcol // 512
            P.op("sync", lambda e, b=b, t=t: e.dma_start(out=h_sb[b][:], in_=h1s[t]), reads=[("h1s", t)], writes=[("h", b)], dma=("l_h", b))
            for half in range(2):
                for fc in range(NFC):
                    P.op("tensor", lambda e, b=b, fc=fc, half=half, lcol=lcol: e.matmul(psA[b][:, half * 512:(half + 1) * 512], lhsT=actT[:, fc, lcol:lcol + 128],
                                                                                      rhs=wout_sb[:, fc, half * 512:(half + 1) * 512], start=(fc == 0), stop=(fc == NFC - 1)),
                         reads=[("actT", fc, j), "wout0", "wout1"], writes=[("psA", b)])
            P.op("vector", lambda e, b=b: e.scalar_tensor_tensor(out=z_sb[b][:], in0=h_sb[b][:], scalar=ALPHA, in1=psA[b][:], op0=ALU.mult, op1=ALU.add),
                 reads=[("h", b), ("psA", b)], writes=[("z", b)])
            layer_norm_tile(b, 2, 3)
            P.op("scalar", lambda e, b=b: e.activation(out=hb_sb[b][:], in_=z_sb[b][:], func=AF.Copy), reads=[("z", b)], writes=[("hb", b)])
            P.op("gpsimd", lambda e, b=b, tm=tm: e.dma_start(out=ho[tm], in_=z_sb[b][:]), reads=[("z", b)], dma=("s_ho", b))
            P.op("gpsimd", lambda e, b=b, tm=tm: e.dma_start(out=hbo[tm], in_=hb_sb[b][:]), reads=[("hb", b)], dma=("s_hbo", b))
    fin = [(s, s.count, "dma") for kk, s in P.dsem.items() if isinstance(kk, tuple) and kk[0] in ("s_ho", "s_hbo")]
    P.wait("gpsimd", fin)
    return P.emit()


NEG = -30000.0
TWO_PI = 2.0 * math.pi


def build_A(variant, NTILE=64, lambda_init=0.2, upto=None):
    assert variant in ("diff", "fox", "mla", "moba")
    S = NTILE * 128
    NQC = NTILE // 4
    NB = S // 256
    P = Prog()
    diff = variant == "diff"; fox = variant == "fox"; mla = variant == "mla"; moba = variant == "moba"
    rot = variant in ("diff", "moba", "mla")
    NFQ = 16 if mla else 8
    NF = {"diff": 384, "fox": 386, "moba": 384, "mla": 672}[variant]
    scale = (96 ** -0.5) if mla else 0.125
    DV = 128 if diff else 64
    NVH = 1 if diff else 2
    OW = 256

    hT = P.dram_in("hT", [8, 128, S], BF16)
    w = P.dram_in("w", [2, 128, 8, NF], BF16) if not mla else P.dram_in("w", [128, 8, NF], BF16)
    o = P.dram_out("o", [S, OW], BF16)
    if rot:
        pos = P.dram_in("pos", [128, NTILE], I32)
        invf = P.dram_in("invf", [128, NFQ], F32)
    if diff:
        lamv = P.dram_in("lamv", [4, 64], F32)
        subg = P.dram_in("subg", [128], F32)
    if fox:
        bf_in = P.dram_in("bf", [4], F32)
        tri = P.dram_in("tri", [128, 128], F32)
        sel = P.dram_in("sel", [128, 2, 128], BF16)
    if mla:
        wuq = P.dram_in("wuq", [2, 128, 3, 192], BF16)
        wukv = P.dram_in("wukv", [2, 128, 2, 256], BF16)
        gq = P.dram_in("gq", [384], F32)
        gkv = P.dram_in("gkv", [256], F32)

    w_sb = P.sb("w_sb", [128, 8, NF], BF16)
    hT_blk = [P.sb(f"hT_blk{i}", [128, 8, 512], BF16) for i in range(2)]
    QT = P.sb("QT", [128, 2 if mla else 1, S], BF16)
    KZ = P.sb("KZ", [128, 2, S], BF16)
    V_sb = P.sb("V_sb", [128, NTILE, NVH, DV + 1], BF16)
    ident = P.sb("ident", [128, 128], BF16)
    identf = P.sb("identf", [128, 128], F32)
    cmf = P.sb("cmf", [128, 512], F32)
    CM = P.sb("CM", [128, 4, 512], BF16)
    qk_sb = [P.sb(f"qk_sb{i}", [128, 4, 128 if mla else 64], BF16) for i in range(2)]
    PT = [P.sb(f"PT{i}", [128, 512], BF16) for i in range(3)]
    o_sb = [P.sb(f"o_sb{i}", [128, 4, 64], BF16) for i in range(2)]
    rec_sb = [P.sb(f"rec_sb{i}", [128, 4, 1], F32) for i in range(2)]
    eps_sb = P.sb("eps_sb", [128, 1], F32)
    if rot:
        pos_sb = P.sb("pos_sb", [128, NTILE], I32)
        posf = P.sb("posf", [128, NTILE], F32)
        invf_sb = P.sb("invf_sb", [128, NFQ], F32)
        ang = P.sb("ang", [128, NTILE, NFQ], F32)
        ang2 = P.sb("ang2", [128, NTILE, NFQ], F32)
        angi = P.sb("angi", [128, NTILE, NFQ], I32)
        cos_sb = P.sb("cos_sb", [128, NTILE, NFQ], F32)
        sin_sb = P.sb("sin_sb", [128, NTILE, NFQ], F32)
        NRH = 4 if not mla else 2
        rt = [[P.sb(f"rt{i}_{k}", [128, NRH, NFQ], F32) for k in range(4)] for i in range(2)]
    if moba:
        km32 = P.sb("km32", [128, NB], F32)
        kmbz = P.sb("kmbz", [128, 2, NB], BF16)
        gate_sb = [P.sb(f"gate_sb{i}", [128, 2, NB], F32) for i in range(2)]
        max8 = [P.sb(f"max8_{i}", [128, 2, 8], F32) for i in range(2)]
        bias_f = [P.sb(f"bias_f{i}", [128, 2, NB], F32) for i in range(2)]
        bias_b = [P.sb(f"bias_b{i}", [128, 128], BF16) for i in range(2)]
        BiasT = P.sb("BiasT", [128, S], BF16)
    if diff:
        lam_sb = P.sb("lam_sb", [128, 4, 64], F32)
        lamt = P.sb("lamt", [128, 8], F32)
        gfull = P.sb("gfull", [128, 128], F32)
        du = [P.sb(f"du{i}", [128, 2, 128], F32) for i in range(2)]
        dv_ = [P.sb(f"dv{i}", [128, 2, 128], F32) for i in range(2)]
        dr = [P.sb(f"dr{i}", [128, 8], F32) for i in range(2)]
        do_sb = [P.sb(f"do_sb{i}", [128, 2, 128], BF16) for i in range(2)]
    if fox:
        flog = P.sb("flog", [128, NTILE, 2], F32)
        bf_sb = P.sb("bf_sb", [128, 4], F32)
        tri_sb = P.sb("tri_sb", [128, 128], F32)
        onesf = P.sb("onesf", [128, 128], F32)
        sel_sb = P.sb("sel_sb", [128, 2, 128], BF16)
        lcs = P.sb("lcs", [128, NTILE, 2], F32)
        tot = P.sb("tot", [128, 2, NTILE], F32)
        excl = P.sb("excl", [128, 2, NTILE], F32)
        cfull = P.sb("cfull", [128, NTILE, 2], F32)
        crel = P.sb("crel", [128, NTILE, 2], F32)
        cparts = P.sb("cparts", [128, NTILE, 128], BF16)
        CQ = P.sb("CQ", [128, S], BF16)
        Bk = P.sb("Bk", [128, 2, NTILE, NQC], F32)
    if mla:
        wuq_sb = P.sb("wuq_sb", [128, 3, 192], BF16)
        wukv_sb = P.sb("wukv_sb", [128, 2, 256], BF16)
        gq_sb = P.sb("gq_sb", [128, 384], F32)
        gkv_sb = P.sb("gkv_sb", [128, 256], F32)
        lat_f = [P.sb(f"lat_f{i}", [128, 640], F32) for i in range(2)]
        lat_b = [P.sb(f"lat_b{i}", [128, 640], BF16) for i in range(2)]
        latT = [P.sb(f"latT{i}", [128, 5, 128], BF16) for i in range(2)]
        ssq = [P.sb(f"ssq{i}", [128, 8], F32) for i in range(2)]
        junk = P.sb("junk", [128, 384], F32)

    B = [P.ps(f"B{i}", [128, 512], F32) for i in range(7)]
    psT = P.ps("psT", [128, 8, 128], BF16)

    P.op("gpsimd", lambda e: e.memset(eps_sb[:], 1e-6), writes=["eps"])
    P.op("gpsimd", lambda e: e.memset(identf[:], 1.0), writes=["identf"])
    P.op("gpsimd", lambda e: e.affine_select(out=identf[:], in_=identf[:], pattern=[[1, 128]], compare_op=ALU.is_equal,
                                             fill=0.0, base=0, channel_multiplier=-1), reads=["identf"], writes=["identf"])
    P.op("gpsimd", lambda e: e.tensor_copy(out=ident[:], in_=identf[:]), reads=["identf"], writes=["ident"])
    for jj in range(4):
        P.op("gpsimd", lambda e: e.memset(cmf[:], 0.0), writes=["cmf"])
        P.op("gpsimd", lambda e, jj=jj: e.affine_select(out=cmf[:], in_=cmf[:], pattern=[[1, 512]], compare_op=ALU.is_ge,
                                                        fill=NEG, base=-128 * jj, channel_multiplier=-1), reads=["cmf"], writes=["cmf"])
        P.op("gpsimd", lambda e, jj=jj: e.tensor_copy(out=CM[:, jj, :], in_=cmf[:]), reads=["cmf"], writes=[("CM", jj)])
    P.op("gpsimd", lambda e: e.memset(V_sb[:, :, :, DV:DV + 1], 1.0), writes=["Vones"])
    P.op("gpsimd", lambda e: e.memset(KZ[:, 0, :], 0.0), writes=["KZzero"])
    P.op("gpsimd", lambda e: e.memset(KZ[:, 1, :], 0.0), writes=["KZzero"])
    if mla:
        for i in range(2):
            P.op("gpsimd", lambda e, i=i: e.memset(qk_sb[i][:], 0.0), writes=[("qkz", i)])

    if rot:
        P.op("sync", lambda e: e.dma_start(out=pos_sb[:], in_=pos), writes=["pos"], dma="l_pos")
        P.op("sync", lambda e: e.dma_start(out=invf_sb[:], in_=invf), writes=["invf"], dma="l_invf")
        P.op("vector", lambda e: e.tensor_copy(out=posf[:], in_=pos_sb[:]), reads=["pos"], writes=["posf"])
        P.op("vector", lambda e: e.tensor_tensor(out=ang[:], in0=posf[:].unsqueeze(2).to_broadcast([128, NTILE, NFQ]),
                                                 in1=invf_sb[:].unsqueeze(1).to_broadcast([128, NTILE, NFQ]), op=ALU.mult),
             reads=["posf", "invf"], writes=["ang"])
        C1 = 6.28125; C2 = TWO_PI - 6.28125
        A_ = ["ang"]; A2 = ["ang2"]
        P.op("vector", lambda e: e.tensor_scalar(out=ang2[:], in0=ang[:], scalar1=1.0 / TWO_PI, scalar2=None, op0=ALU.mult), reads=A_, writes=A2)
        P.op("vector", lambda e: e.tensor_copy(out=angi[:], in_=ang2[:]), reads=A2, writes=["angi"])
        P.op("vector", lambda e: e.tensor_copy(out=ang2[:], in_=angi[:]), reads=["angi"], writes=A2)
        P.op("vector", lambda e: e.scalar_tensor_tensor(out=ang[:], in0=ang2[:], scalar=-C1, in1=ang[:], op0=ALU.mult, op1=ALU.add), reads=A_ + A2, writes=A_)
        P.op("vector", lambda e: e.scalar_tensor_tensor(out=ang[:], in0=ang2[:], scalar=-C2, in1=ang[:], op0=ALU.mult, op1=ALU.add), reads=A_ + A2, writes=A_)
        P.op("vector", lambda e: e.tensor_scalar(out=ang2[:], in0=ang[:], scalar1=math.pi, scalar2=None, op0=ALU.is_gt), reads=A_, writes=A2)
        P.op("vector", lambda e: e.scalar_tensor_tensor(out=ang[:], in0=ang2[:], scalar=-TWO_PI, in1=ang[:], op0=ALU.mult, op1=ALU.add), reads=A_ + A2, writes=A_)
        P.op("vector", lambda e: e.tensor_scalar(out=ang2[:], in0=ang[:], scalar1=-math.pi, scalar2=None, op0=ALU.is_lt), reads=A_, writes=A2)
        P.op("vector", lambda e: e.scalar_tensor_tensor(out=ang[:], in0=ang2[:], scalar=TWO_PI, in1=ang[:], op0=ALU.mult, op1=ALU.add), reads=A_ + A2, writes=A_)
        P.op("scalar", lambda e: e.activation(out=sin_sb[:], in_=ang[:], func=AF.Sin), reads=A_, writes=["sin"])
        P.op("vector", lambda e: e.tensor_scalar(out=ang[:], in0=ang[:], scalar1=0.5 * math.pi, scalar2=None, op0=ALU.add), reads=A_, writes=A_)
        P.op("vector", lambda e: e.tensor_scalar(out=ang2[:], in0=ang[:], scalar1=math.pi, scalar2=None, op0=ALU.is_gt), reads=A_, writes=A2)
        P.op("vector", lambda e: e.scalar_tensor_tensor(out=ang[:], in0=ang2[:], scalar=-TWO_PI, in1=ang[:], op0=ALU.mult, op1=ALU.add), reads=A_ + A2, writes=A_)
        P.op("scalar", lambda e: e.activation(out=cos_sb[:], in_=ang[:], func=AF.Sin), reads=A_, writes=["cos"])

    if diff:
        P.op("sync", lambda e: e.dma_start(out=lam_sb[:], in_=lamv.partition_broadcast(128)), writes=["lamv"], dma="l_lam")
        P.op("sync", lambda e: e.dma_start(out=gfull[:], in_=subg.partition_broadcast(128)), writes=["gfull"], dma="l_g")
        P.op("vector", lambda e: e.tensor_tensor(out=lam_sb[:, 0, :], in0=lam_sb[:, 0, :], in1=lam_sb[:, 1, :], op=ALU.mult), reads=["lamv"], writes=["lamv"])
        P.op("vector", lambda e: e.tensor_tensor(out=lam_sb[:, 2, :], in0=lam_sb[:, 2, :], in1=lam_sb[:, 3, :], op=ALU.mult), reads=["lamv"], writes=["lamv"])
        P.op("vector", lambda e: e.tensor_reduce(out=lamt[:, 0:1], in_=lam_sb[:, 0, :], axis=AX.X, op=ALU.add), reads=["lamv"], writes=["lamt"])
        P.op("vector", lambda e: e.tensor_reduce(out=lamt[:, 1:2], in_=lam_sb[:, 2, :], axis=AX.X, op=ALU.add), reads=["lamt", "lamv"], writes=["lamt"])
        P.op("scalar", lambda e: e.activation(out=lamt[:, 2:4], in_=lamt[:, 0:2], func=AF.Exp), reads=["lamt"], writes=["lamt"])
        P.op("vector", lambda e: e.tensor_tensor(out=lamt[:, 4:5], in0=lamt[:, 3:4], in1=lamt[:, 2:3], op=ALU.subtract), reads=["lamt"], writes=["lamt"])
        P.op("vector", lambda e: e.tensor_scalar(out=lamt[:, 5:6], in0=lamt[:, 4:5], scalar1=-lambda_init, scalar2=None, op0=ALU.add), reads=["lamt"], writes=["nlam"])
        P.op("vector", lambda e: e.tensor_scalar(out=gfull[:], in0=gfull[:], scalar1=(1.0 - lambda_init), scalar2=None, op0=ALU.mult), reads=["gfull"], writes=["gfull"])
    if fox:
        P.op("sync", lambda e: e.dma_start(out=bf_sb[:], in_=bf_in.partition_broadcast(128)), writes=["bf"], dma="l_bf")
        P.op("sync", lambda e: e.dma_start(out=tri_sb[:], in_=tri), writes=["tri"], dma="l_tri")
        P.op("sync", lambda e: e.dma_start(out=sel_sb[:], in_=sel), writes=["sel"], dma="l_sel")
        P.op("gpsimd", lambda e: e.memset(onesf[:], 1.0), writes=["onesf"])
        P.op("gpsimd", lambda e: e.memset(cparts[:], 0.0), writes=["cpz"])
    if mla:
        P.op("sync", lambda e: e.dma_start(out=w_sb[:], in_=w), writes=["w"], dma="l_w")
        P.op("sync", lambda e: e.dma_start(out=gq_sb[:], in_=gq.partition_broadcast(128)), writes=["gq"], dma="l_gq")
        P.op("sync", lambda e: e.dma_start(out=gkv_sb[:], in_=gkv.partition_broadcast(128)), writes=["gkv"], dma="l_gkv")
    if moba:
        for i in range(2):
            P.op("gpsimd", lambda e, i=i: e.memset(bias_b[i][:], 0.0), writes=[("bias_b", i)])
        P.op("gpsimd", lambda e: e.memset(kmbz[:], 0.0), writes=["kmbz"])
    if upto == 'setup':
        return P.emit()

    def rotary(src, dst, tt, nh, b, half, kd):
        cb = cos_sb[:, tt:tt + 1, 0:half].to_broadcast([128, nh, half])
        sb_ = sin_sb[:, tt:tt + 1, 0:half].to_broadcast([128, nh, half])
        x1 = src[:, :, 0:half]; x2 = src[:, :, half:2 * half]
        t = [rt[b][k][:, 0:nh, 0:half] for k in range(4)]
        R = [("rt", b, k) for k in range(4)]
        P.op("vector", lambda e: e.tensor_tensor(out=t[0], in0=x1, in1=cb, op=ALU.mult), reads=kd["r"] + ["cos"], writes=[R[0]])
        P.op("vector", lambda e: e.tensor_tensor(out=t[1], in0=x2, in1=sb_, op=ALU.mult), reads=kd["r"] + ["sin"], writes=[R[1]])
        P.op("vector", lambda e: e.tensor_tensor(out=dst[:, :, 0:half], in0=t[0], in1=t[1], op=ALU.subtract), reads=[R[0], R[1]], writes=kd["w"])
        P.op("vector", lambda e: e.tensor_tensor(out=t[2], in0=x2, in1=cb, op=ALU.mult), reads=kd["r"] + ["cos"], writes=[R[2]])
        P.op("vector", lambda e: e.tensor_tensor(out=t[3], in0=x1, in1=sb_, op=ALU.mult), reads=kd["r"] + ["sin"], writes=[R[3]])
        P.op("vector", lambda e: e.tensor_tensor(out=dst[:, :, half:2 * half], in0=t[2], in1=t[3], op=ALU.add), reads=[R[2], R[3]], writes=kd["w"])

    cnt = dict(tile=0, it=0, ci=0, oc=0)

    def projection(hp):
        if not mla:
            P.op("sync", lambda e: e.dma_start(out=w_sb[:], in_=w[hp]), writes=["w"], dma="l_w")
        else:
            P.op("sync", lambda e: e.dma_start(out=wuq_sb[:], in_=wuq[hp]), writes=["wuq"], dma="l_wuq")
            P.op("sync", lambda e: e.dma_start(out=wukv_sb[:], in_=wukv[hp]), writes=["wukv"], dma="l_wukv")
        for tt in range(NTILE):
            g = cnt["tile"]; cnt["tile"] += 1
            blk = g // 4; lt = tt % 4; hb = blk % 2; b = g % 2
            if lt == 0:
                tb = tt // 4
                P.op("sync", lambda e, hb=hb, tb=tb: e.dma_start(out=hT_blk[hb][:], in_=hT[:, :, tb * 512:(tb + 1) * 512].rearrange("c p t -> p c t")),
                     writes=[("hTb", hb)], dma=("l_hT", hb))
            QK = ("qk", b)
            qkv = qk_sb[b]
            if not mla:
                p0 = B[b]; K0 = ("B", b)
                for c in range(8):
                    P.op("tensor", lambda e, c=c, hb=hb, lt=lt, p0=p0: e.matmul(p0[:, 0:NF], lhsT=hT_blk[hb][:, c, lt * 128:(lt + 1) * 128], rhs=w_sb[:, c, :],
                                                                                start=(c == 0), stop=(c == 7)), reads=[("hTb", hb), "w"], writes=[K0])
                p0v = p0[:, 0:256].rearrange("p (h d) -> p h d", d=64)
                P.op("scalar", lambda e, p0v=p0v, qkv=qkv: e.activation(out=qkv[:], in_=p0v, func=AF.Copy), reads=[K0], writes=[QK])
                P.op("scalar", lambda e, tt=tt, p0=p0: e.activation(out=V_sb[:, tt, :, 0:DV], in_=p0[:, 256:384].rearrange("p (h d) -> p h d", d=DV), func=AF.Copy),
                     reads=[K0], writes=[("V", tt)])
                if rot:
                    rotary(p0v, qkv, tt, 4, b, 8, dict(r=[K0, QK, ("V", tt)], w=[QK]))
                if fox:
                    P.op("vector", lambda e, tt=tt, p0=p0: e.tensor_tensor(out=flog[:, tt, :], in0=p0[:, 384:386], in1=bf_sb[:, 2 * hp:2 * hp + 2], op=ALU.add),
                         reads=[K0, "bf", ("V", tt), QK], writes=[("flog", tt)])
                for j in range(2):
                    P.op("tensor", lambda e, j=j, qkv=qkv: e.transpose(psT[:, j, :], qkv[:, 2 * j:2 * j + 2, :].rearrange("p h d -> p (h d)"), ident[:]),
                         reads=[QK, "ident"], writes=["psT"])
                P.op("vector", lambda e, tt=tt: e.tensor_copy(out=QT[:, 0, tt * 128:(tt + 1) * 128], in_=psT[:, 0, :]), reads=["psT"], writes=[("QT", tt)])
                P.op("scalar", lambda e, tt=tt: e.activation(out=KZ[0:64, 0, tt * 128:(tt + 1) * 128], in_=psT[0:64, 1, :], func=AF.Copy),
                     reads=["psT", "KZzero", ("QT", tt)], writes=[("KZ", tt)])
                P.op("scalar", lambda e, tt=tt: e.activation(out=KZ[64:128, 1, tt * 128:(tt + 1) * 128], in_=psT[64:128, 1, :], func=AF.Copy),
                     reads=["psT", "KZzero", ("QT", tt)], writes=[("KZ", tt)])
            else:
                p0 = B[0]; p1 = B[1]; K0 = ("B", 0); K1 = ("B", 1)
                for c in range(8):
                    P.op("tensor", lambda e, c=c, hb=hb, lt=lt: e.matmul(p0[:, :], lhsT=hT_blk[hb][:, c, lt * 128:(lt + 1) * 128], rhs=w_sb[:, c, 0:512],
                                                                         start=(c == 0), stop=(c == 7)), reads=[("hTb", hb), "w"], writes=[K0])
                for c in range(8):
                    P.op("tensor", lambda e, c=c, hb=hb, lt=lt: e.matmul(p1[:, 0:160], lhsT=hT_blk[hb][:, c, lt * 128:(lt + 1) * 128], rhs=w_sb[:, c, 512:672],
                                                                         start=(c == 0), stop=(c == 7)), reads=[("hTb", hb), "w"], writes=[K1])
                lf = lat_f[b]; lb = lat_b[b]; LF = ("latf", b); LB = ("latb", b); SS = ("ssq", b)
                P.op("scalar", lambda e, lf=lf: e.activation(out=lf[:, 0:512], in_=p0[:, :], func=AF.Copy), reads=[K0], writes=[(LF, 0)])
                P.op("scalar", lambda e, lf=lf: e.activation(out=lf[:, 512:640], in_=p1[:, 0:128], func=AF.Copy), reads=[K1], writes=[(LF, 1)])
                sq = ssq[b]
                P.op("vector", lambda e, lf=lf: e.tensor_tensor(out=junk[:, 0:384], in0=lf[:, 0:384], in1=lf[:, 0:384], op=ALU.mult), reads=[(LF, 0)], writes=["junk"])
                P.op("vector", lambda e, sq=sq: e.tensor_reduce(out=sq[:, 0:1], in_=junk[:, 0:384], axis=AX.X, op=ALU.add), reads=["junk"], writes=[(SS, 0)])
                P.op("vector", lambda e, lf=lf: e.tensor_tensor(out=junk[:, 0:256], in0=lf[:, 384:640], in1=lf[:, 384:640], op=ALU.mult), reads=[(LF, 0), (LF, 1)], writes=["junk"])
                P.op("vector", lambda e, sq=sq: e.tensor_reduce(out=sq[:, 1:2], in_=junk[:, 0:256], axis=AX.X, op=ALU.add), reads=["junk"], writes=[(SS, 1)])
                P.op("scalar", lambda e, sq=sq: e.activation(out=sq[:, 2:3], in_=sq[:, 0:1], func=AF.Ln, bias=eps_sb[:, 0:1], scale=1.0 / 384), reads=[(SS, 0), "eps"], writes=[(SS, 2)])
                P.op("scalar", lambda e, sq=sq: e.activation(out=sq[:, 3:4], in_=sq[:, 1:2], func=AF.Ln, bias=eps_sb[:, 0:1], scale=1.0 / 256), reads=[(SS, 1), "eps"], writes=[(SS, 3)])
                P.op("scalar", lambda e, sq=sq: e.activation(out=sq[:, 4:6], in_=sq[:, 2:4], func=AF.Exp, scale=-0.5), reads=[(SS, 2), (SS, 3)], writes=[(SS, 4)])
                P.op("vector", lambda e, lf=lf, lb=lb, sq=sq: e.scalar_tensor_tensor(out=lb[:, 0:384], in0=lf[:, 0:384], scalar=sq[:, 4:5], in1=gq_sb[:], op0=ALU.mult, op1=ALU.mult),
                     reads=[(LF, 0), (SS, 4), "gq"], writes=[(LB, 0)])
                P.op("vector", lambda e, lf=lf, lb=lb, sq=sq: e.scalar_tensor_tensor(out=lb[:, 384:640], in0=lf[:, 384:640], scalar=sq[:, 5:6], in1=gkv_sb[:], op0=ALU.mult, op1=ALU.mult),
                     reads=[(LF, 0), (LF, 1), (SS, 4), "gkv"], writes=[(LB, 1)])
                rotary(p1[:, 128:160].rearrange("p (h d) -> p h d", h=1), qkv[:, 2:3, 64:96], tt, 1, b, 16, dict(r=[K1, (LF, 1), ("qkz", b)], w=[(QK, "kr")]))
                P.op("vector", lambda e, qkv=qkv: e.tensor_copy(out=qkv[:, 3:4, 64:96], in_=qkv[:, 2:3, 64:96]), reads=[(QK, "kr")], writes=[(QK, "kr2")])
                for c in range(5):
                    P.op("tensor", lambda e, c=c, lb=lb: e.transpose(psT[:, c, :], lb[:, c * 128:(c + 1) * 128], ident[:]),
                         reads=[(LB, 0), (LB, 1), "ident"], writes=["psT"])
                lT = latT[b]
                P.op("scalar", lambda e, lT=lT: e.activation(out=lT[:], in_=psT[:, 0:5, :], func=AF.Copy), reads=["psT"], writes=[("latT", b)])
                pq = B[2]; KQ = ("B", 2); pkv = B[3]; KKV = ("B", 3)
                for c in range(3):
                    P.op("tensor", lambda e, c=c, lT=lT: e.matmul(pq[:, 0:192], lhsT=lT[:, c, :], rhs=wuq_sb[:, c, :], start=(c == 0), stop=(c == 2)),
                         reads=[("latT", b), "wuq"], writes=[KQ])
                for c in range(2):
                    P.op("tensor", lambda e, c=c, lT=lT: e.matmul(pkv[:, 0:256], lhsT=lT[:, 3 + c, :], rhs=wukv_sb[:, c, :], start=(c == 0), stop=(c == 1)),
                         reads=[("latT", b), "wukv"], writes=[KKV])
                pqv = pq[:, 0:192].rearrange("p (h d) -> p h d", d=96)
                pkvv = pkv[:, 0:256].rearrange("p (h d) -> p h d", d=128)
                P.op("scalar", lambda e, qkv=qkv, pqv=pqv: e.activation(out=qkv[:, 0:2, 0:64], in_=pqv[:, :, 0:64], func=AF.Copy), reads=[KQ, ("qkz", b)], writes=[(QK, "qn")])
                P.op("scalar", lambda e, qkv=qkv, pkvv=pkvv: e.activation(out=qkv[:, 2:4, 0:64], in_=pkvv[:, :, 0:64], func=AF.Copy), reads=[KKV, ("qkz", b)], writes=[(QK, "kn")])
                P.op("scalar", lambda e, tt=tt, pkvv=pkvv: e.activation(out=V_sb[:, tt, :, 0:64], in_=pkvv[:, :, 64:128], func=AF.Copy), reads=[KKV], writes=[("V", tt)])
                rotary(pqv[:, :, 64:96], qkv[:, 0:2, 64:96], tt, 2, b, 16, dict(r=[KQ, (QK, "qn")], w=[(QK, "qr")]))
                RD = [(QK, x) for x in ("qn", "kn", "qr", "kr", "kr2")]
                for j in range(4):
                    P.op("tensor", lambda e, j=j, qkv=qkv: e.transpose(psT[:, j, :], qkv[:, j, :], ident[:]), reads=RD + ["ident"], writes=["psT"])
                P.op("vector", lambda e, tt=tt: e.tensor_copy(out=QT[:, :, tt * 128:(tt + 1) * 128], in_=psT[:, 0:2, :]), reads=["psT"], writes=[("QT", tt)])
                P.op("scalar", lambda e, tt=tt: e.activation(out=KZ[:, :, tt * 128:(tt + 1) * 128], in_=psT[:, 2:4, :], func=AF.Copy), reads=["psT", ("QT", tt), "KZzero"], writes=[("KZ", tt)])

    def prologue(hp):
        ALLK = [("KZ", tt) for tt in range(NTILE)]
        if fox:
            FL = [("flog", tt) for tt in range(NTILE)]
            P.op("scalar", lambda e: e.activation(out=flog[:], in_=flog[:], func=AF.Exp, scale=-1.0), reads=FL, writes=["flog_all"])
            P.op("scalar", lambda e: e.activation(out=flog[:], in_=flog[:], func=AF.Ln, bias=1.0, scale=1.0), reads=["flog_all"], writes=["flog_all"])
            P.op("vector", lambda e: e.tensor_scalar(out=flog[:], in0=flog[:], scalar1=-1.0, scalar2=None, op0=ALU.mult), reads=["flog_all"], writes=["flog_all"])
            flat = flog[:].rearrange("p t h -> p (t h)")
            nfl = NTILE * 2
            P.op("tensor", lambda e: e.matmul(B[6][:, 0:nfl], lhsT=tri_sb[:], rhs=flat, start=True, stop=True), reads=["flog_all", "tri"], writes=[("B", 6)])
            P.op("vector", lambda e: e.tensor_copy(out=lcs[:].rearrange("p t h -> p (t h)"), in_=B[6][:, 0:nfl]), reads=[("B", 6)], writes=["lcs"])
            P.op("tensor", lambda e: e.matmul(B[6][:, 0:nfl], lhsT=onesf[:], rhs=flat, start=True, stop=True), reads=["flog_all", "onesf"], writes=[("B", 6)])
            P.op("vector", lambda e: e.tensor_copy(out=tot[:].rearrange("p h t -> p t h"), in_=B[6][:, 0:nfl].rearrange("p (t h) -> p t h", h=2)),
                 reads=[("B", 6)], writes=["tot"])
            for hh in range(2):
                P.op("vector", lambda e, hh=hh: e.tensor_tensor_scan(out=excl[:, hh, :], data0=tot[:, hh, :], data1=tot[:, hh, :], initial=0.0, op0=ALU.add, op1=ALU.bypass),
                     reads=["tot"], writes=[("excl", hh)])
                P.op("vector", lambda e, hh=hh: e.tensor_tensor(out=excl[:, hh, :], in0=excl[:, hh, :], in1=tot[:, hh, :], op=ALU.subtract),
                     reads=["tot", ("excl", hh)], writes=[("excl", hh)])
            EX = [("excl", hh) for hh in range(2)]
            exv = excl[:].rearrange("p h t -> p t h")
            P.op("vector", lambda e: e.tensor_tensor(out=cfull[:], in0=lcs[:], in1=exv, op=ALU.add), reads=["lcs"] + EX, writes=["cfull"])
            crefv = excl[:].rearrange("p h (q f) -> p q f h", f=4)[:, :, 0:1, :].to_broadcast([128, NQC, 4, 2])
            P.op("vector", lambda e: e.tensor_tensor(out=crel[:].rearrange("p (q f) h -> p q f h", f=4), in0=cfull[:].rearrange("p (q f) h -> p q f h", f=4),
                                                     in1=crefv, op=ALU.subtract), reads=["cfull"] + EX, writes=["crel"])
            P.op("vector", lambda e: e.tensor_scalar(out=crel[:], in0=crel[:], scalar1=8.0, scalar2=None, op0=ALU.mult), reads=["crel"], writes=["crel"])
            cpv = cparts[:, :, 0:6].rearrange("p t (h k) -> p t h k", k=3)
            for part in range(3):
                P.op("vector", lambda e, part=part: e.tensor_copy(out=cpv[:, :, :, part], in_=crel[:]), reads=["crel", "cpz"], writes=[("cparts", part)])
                if part < 2:
                    P.op("vector", lambda e, part=part: e.tensor_tensor(out=crel[:], in0=crel[:], in1=cpv[:, :, :, part], op=ALU.subtract),
                         reads=["crel", ("cparts", part)], writes=["crel"])
            for tt in range(NTILE):
                P.op("tensor", lambda e, tt=tt: e.transpose(psT[:, 4, :], cparts[:, tt, :], ident[:]), reads=[("cparts", 0), ("cparts", 1), ("cparts", 2), "ident"], writes=["psT4"])
                P.op("vector", lambda e, tt=tt: e.tensor_copy(out=CQ[:, tt * 128:(tt + 1) * 128], in_=psT[:, 4, :]), reads=["psT4"], writes=[("CQ", tt)])
            for hh in range(2):
                negc = cfull[:, :, hh:hh + 1].to_broadcast([128, NTILE, NQC])
                crf = excl[:, hh, :].rearrange("p (q f) -> p q f", f=4)[:, :, 0].unsqueeze(1).to_broadcast([128, NTILE, NQC])
                P.op("vector", lambda e, hh=hh, negc=negc, crf=crf: e.tensor_tensor(out=Bk[:, hh, :, :], in0=crf, in1=negc, op=ALU.subtract),
                     reads=["cfull"] + EX, writes=[("Bk", hh)])
                P.op("vector", lambda e, hh=hh: e.tensor_scalar(out=Bk[:, hh, :, :], in0=Bk[:, hh, :, :], scalar1=-300.0, scalar2=None, op0=ALU.max),
                     reads=[("Bk", hh)], writes=[("Bk", hh)])
        if moba:
            for m in range(2):
                rb = m * 64
                P.op("vector", lambda e, m=m: e.tensor_reduce(out=km32[:], in_=KZ[:, m, :].rearrange("p (n k) -> p n k", k=256), axis=AX.X, op=ALU.add),
                     reads=ALLK, writes=["km32"])
                P.op("vector", lambda e, m=m, rb=rb: e.tensor_scalar(out=kmbz[rb:rb + 64, m, :], in0=km32[rb:rb + 64, :], scalar1=1.0 / 256, scalar2=None, op0=ALU.mult),
                     reads=["km32"], writes=["kmbz"])
            for i in range(2):
                P.op("gpsimd", lambda e, i=i: e.memset(gate_sb[i][:], -1e30), writes=[("gate", i)])
            for tt in range(NTILE):
                cur = tt // 2; gb = tt % 2
                G = ("gate", gb)
                if cur > 0:
                    for m in range(2):
                        P.op("tensor", lambda e, m=m, tt=tt: e.matmul(B[6][:, m * NB:(m + 1) * NB], lhsT=QT[:, 0, tt * 128:(tt + 1) * 128], rhs=kmbz[:, m, :], start=True, stop=True),
                             reads=[("QT", tt), "kmbz"], writes=[("B", 6)])
                    P.op("vector", lambda e, gb=gb, cur=cur: e.tensor_copy(out=gate_sb[gb][:, :, 0:cur], in_=B[6][:, 0:2 * NB].rearrange("p (m n) -> p m n", n=NB)[:, :, 0:cur]),
                         reads=[("B", 6)], writes=[G])
                for m in range(2):
                    P.op("vector", lambda e, gb=gb, m=m: e.max(out=max8[gb][:, m, :], in_=gate_sb[gb][:, m, :]), reads=[G], writes=[("max8", gb, m)])
                P.op("vector", lambda e, gb=gb: e.tensor_tensor(out=bias_f[gb][:], in0=gate_sb[gb][:], in1=max8[gb][:, :, 2:3].to_broadcast([128, 2, NB]), op=ALU.is_lt),
                     reads=[G] + [("max8", gb, m) for m in range(2)], writes=[("bias_f", gb)])
                P.op("vector", lambda e, gb=gb, cur=cur: e.memset(bias_f[gb][:, :, cur:cur + 1], 0.0), reads=[("bias_f", gb)], writes=[("bias_f", gb)])
                P.op("vector", lambda e, gb=gb: e.tensor_scalar(out=bias_b[gb][:, 0:2 * NB], in0=bias_f[gb][:].rearrange('p m n -> p (m n)'), scalar1=NEG, scalar2=None, op0=ALU.mult),
                     reads=[("bias_f", gb)], writes=[("bias_b", gb)])
                P.op("tensor", lambda e, gb=gb: e.transpose(psT[:, 4, :], bias_b[gb][:], ident[:]), reads=[("bias_b", gb), "ident"], writes=["psT4"])
                P.op("scalar", lambda e, tt=tt: e.activation(out=BiasT[:, tt * 128:(tt + 1) * 128], in_=psT[:, 4, :], func=AF.Copy), reads=["psT4"], writes=[("BiasT", tt)])

    SB = [0, 1, 2]

    def attention(hp):
        if diff:
            order = [(qc, c) for qc in range(NQC) for c in range(2)]
        else:
            order = [(qc, m) for m in range(2) for qc in range(NQC)]
        its = []
        for (qc, m) in order:
            ci = cnt["ci"]; cnt["ci"] += 1
            nk = 4 * (qc + 1)
            for kt in range(nk):
                its.append(dict(ci=ci, m=m, qc=qc, kt=kt, nk=nk))
        NI = len(its)
        LA = 2
        base_it = cnt["it"]; cnt["it"] += NI

        def emit_qk(i):
            it = its[i]; m = it["m"]; qc = it["qc"]; kt = it["kt"]
            gi = base_it + i
            sb = SB[gi % 3]; KS = ("B", sb)
            diag = kt >= 4 * qc
            extra = fox or moba
            rdq = [("QT", 4 * qc + x) for x in range(4)] + [("KZ", kt)]
            qsl = QT[:, m if mla else 0, qc * 512:(qc + 1) * 512]
            P.op("tensor", lambda e: e.matmul(B[sb][:, :], lhsT=KZ[:, m, kt * 128:(kt + 1) * 128], rhs=qsl, start=True, stop=not (diag or extra)), reads=rdq, writes=[KS])
            if fox:
                P.op("tensor", lambda e: e.matmul(B[sb][:, :], lhsT=sel_sb[:, m, :], rhs=CQ[:, qc * 512:(qc + 1) * 512], start=False, stop=not diag),
                     reads=["sel"] + [("CQ", 4 * qc + x) for x in range(4)], writes=[KS])
            if moba:
                col = m * NB + kt // 2
                P.op("tensor", lambda e: e.matmul(B[sb][:, :], lhsT=ident[:, col:col + 1].to_broadcast([128, 128]), rhs=BiasT[:, qc * 512:(qc + 1) * 512], start=False, stop=not diag),
                     reads=["ident"] + [("BiasT", 4 * qc + x) for x in range(4)], writes=[KS])
            if diag:
                jj = kt - 4 * qc
                P.op("tensor", lambda e: e.matmul(B[sb][:, :], lhsT=ident[:], rhs=CM[:, jj, :], start=False, stop=True), reads=["ident", ("CM", jj)], writes=[KS])

        def acc_view(it, qs):
            if diff:
                bank = 3 + 2 * it["m"] + qs // 2
                return B[bank][:, 0:258].rearrange("p (q d) -> p q d", d=129)[:, qs % 2, :], ("B", bank)
            bank = 3 + (it["ci"] % 3)
            return B[bank][:, 0:260].rearrange("p (q d) -> p q d", d=65)[:, qs, :], ("B", bank)

        def emit_exp_pv(i):
            it = its[i]; m = it["m"]; qc = it["qc"]; kt = it["kt"]; nk = it["nk"]
            gi = base_it + i
            sb = SB[gi % 3]; KS = ("B", sb); pb = gi % 3; KP = ("PT", pb)
            if fox:
                bias_ap = Bk[:, m, kt, qc:qc + 1]
                P.op("scalar", lambda e: e.activation(out=PT[pb][:], in_=B[sb][:, :], func=AF.Exp, bias=bias_ap, scale=scale), reads=[KS, ("Bk", m)], writes=[KP])
            else:
                P.op("scalar", lambda e: e.activation(out=PT[pb][:], in_=B[sb][:, :], func=AF.Exp, scale=scale), reads=[KS], writes=[KP])
            vh = 0 if diff else m
            for qs in range(4):
                jj = kt - 4 * qc
                if jj > qs:
                    continue
                av, KA = acc_view(it, qs)
                last_kt = 4 * qc + qs
                P.op("tensor", lambda e, qs=qs, av=av: e.matmul(av, lhsT=PT[pb][:, qs * 128:(qs + 1) * 128], rhs=V_sb[:, kt, vh, :], start=(kt == 0 and (qs % 2 == 0 if diff else qs == 0)), stop=(kt == last_kt)),
                     reads=[KP, ("V", kt), "Vones"], writes=[KA])
            if kt == nk - 1:
                epilogue(it)

        def epilogue(it):
            m = it["m"]; qc = it["qc"]
            if not diff:
                oc = cnt["oc"]; cnt["oc"] += 1
                ob = oc % 2
                bank = 3 + (it["ci"] % 3)
                av = B[bank][:, 0:260].rearrange("p (q d) -> p q d", d=65)
                P.op("vector", lambda e: e.reciprocal(out=rec_sb[ob][:], in_=av[:, :, 64:65]), reads=[("B", bank)], writes=[("rec", ob)])
                P.op("vector", lambda e: e.tensor_tensor(out=o_sb[ob][:], in0=av[:, :, 0:64], in1=rec_sb[ob][:].to_broadcast([128, 4, 64]), op=ALU.mult),
                     reads=[("B", bank), ("rec", ob)], writes=[("o_sb", ob)])
                col = (2 * hp + m) * 64
                P.op("sync", lambda e: e.dma_start(out=o[qc * 512:(qc + 1) * 512, col:col + 64].rearrange("(q p) d -> p q d", p=128), in_=o_sb[ob][:]),
                     reads=[("o_sb", ob)], dma=("s_o", ob))
                return
            if m == 0:
                return
            for g in range(2):
                cnt["oc"] += 1
                ob = cnt["oc"] % 2
                a1 = B[3 + g][:, 0:258].rearrange("p (q d) -> p q d", d=129); K1 = ("B", 3 + g)
                a2 = B[5 + g][:, 0:258].rearrange("p (q d) -> p q d", d=129); K2 = ("B", 5 + g)
                r = dr[ob]; R = ("dr", ob)
                P.op("vector", lambda e, r=r, a1=a1: e.reciprocal(out=r[:, 0:2], in_=a1[:, :, 128]), reads=[K1], writes=[(R, 0)])
                P.op("vector", lambda e, r=r, a2=a2: e.reciprocal(out=r[:, 2:4], in_=a2[:, :, 128]), reads=[K2], writes=[(R, 1)])
                P.op("vector", lambda e, r=r: e.tensor_scalar(out=r[:, 2:4], in0=r[:, 2:4], scalar1=lamt[:, 5:6], scalar2=None, op0=ALU.mult), reads=[(R, 1), "nlam"], writes=[(R, 1)])
                u = du[ob]; v = dv_[ob]; U = ("du", ob); Vk = ("dv", ob)
                P.op("vector", lambda e, u=u, a1=a1, r=r: e.tensor_tensor(out=u[:], in0=a1[:, :, 0:128], in1=r[:, 0:2].unsqueeze(2).to_broadcast([128, 2, 128]), op=ALU.mult),
                     reads=[K1, (R, 0)], writes=[U])
                P.op("vector", lambda e, v=v, a2=a2, r=r: e.tensor_tensor(out=v[:], in0=a2[:, :, 0:128], in1=r[:, 2:4].unsqueeze(2).to_broadcast([128, 2, 128]), op=ALU.mult),
                     reads=[K2, (R, 1)], writes=[Vk])
                P.op("gpsimd", lambda e, u=u, v=v: e.tensor_tensor(out=u[:], in0=u[:], in1=v[:], op=ALU.add), reads=[U, Vk], writes=[U])
                P.op("gpsimd", lambda e, u=u, v=v: e.tensor_tensor(out=v[:], in0=u[:], in1=u[:], op=ALU.mult), reads=[U], writes=[Vk])
                P.op("vector", lambda e, v=v, r=r: e.tensor_reduce(out=r[:, 4:6], in_=v[:], axis=AX.X, op=ALU.add), reads=[Vk], writes=[(R, 2)])
                P.op("scalar", lambda e, r=r: e.activation(out=r[:, 4:6], in_=r[:, 4:6], func=AF.Ln, bias=eps_sb[:, 0:1], scale=1.0 / 128), reads=[(R, 2), "eps"], writes=[(R, 2)])
                P.op("scalar", lambda e, r=r: e.activation(out=r[:, 6:8], in_=r[:, 4:6], func=AF.Exp, scale=-0.5), reads=[(R, 2)], writes=[(R, 3)])
                P.op("vector", lambda e, u=u, r=r: e.tensor_tensor(out=u[:], in0=u[:], in1=r[:, 6:8].unsqueeze(2).to_broadcast([128, 2, 128]), op=ALU.mult), reads=[U, (R, 3)], writes=[U])
                dob = do_sb[ob]
                P.op("gpsimd", lambda e, u=u, dob=dob: e.tensor_tensor(out=dob[:], in0=u[:], in1=gfull[:].unsqueeze(1).to_broadcast([128, 2, 128]), op=ALU.mult),
                     reads=[U, "gfull"], writes=[("do", ob)])
                r0 = qc * 512 + g * 256
                P.op("sync", lambda e, dob=dob, r0=r0: e.dma_start(out=o[r0:r0 + 256, hp * 128:(hp + 1) * 128].rearrange("(q p) d -> p q d", p=128), in_=dob[:]),
                     reads=[("do", ob)], dma=("s_o", ob))

        for i in range(NI + LA):
            if i < NI:
                emit_qk(i)
            if i - LA >= 0:
                emit_exp_pv(i - LA)

    for hp in range(2):
        projection(hp)
        if upto == 'proj':
            continue
        prologue(hp)
        if upto == 'prologue':
            continue
        attention(hp)
    P.wait("sync", [(s, s.count, "dma") for kk, s in P.dsem.items() if isinstance(kk, tuple) and kk[0] == "s_o"])
    return P.emit()


def inv_freq(rot_dim):
    return (np.float32(500000.0) ** (-np.arange(0, rot_dim, 2, dtype=np.float32) / np.float32(rot_dim))).astype(np.float32)

def w_layout(wcols):
    return np.ascontiguousarray(wcols.reshape(8, 128, -1).transpose(1, 0, 2))

def prep_A(variant, hb, posb, wts, hg, NTILE=64):
    S = NTILE * 128
    d = {"hT": np.ascontiguousarray(hb.T.reshape(8, 128, S))}
    pos_l = np.ascontiguousarray(posb.reshape(NTILE, 128).T.astype(np.int32))
    def qkv_cols(w, h0, n, extra=None):
        out = []
        for hp in range(2):
            a = (h0 + hp * n)
            cols = [w[:, a * 64:(a + n) * 64], w[:, 1024 + a * 64:1024 + (a + n) * 64], w[:, 2048 + a * 64:2048 + (a + n) * 64]]
            if extra is not None:
                cols.append(w[:, 3072 + a:3072 + a + n])
            out.append(w_layout(np.concatenate(cols, 1)))
        return np.stack(out)
    if variant == "diff":
        d["w"] = qkv_cols(wts["w_qkv"], 4 * hg, 2)
        d["pos"] = pos_l; d["invf"] = np.tile(inv_freq(16)[None, :], (128, 1))
        d["lamv"] = np.stack([wts["lq1"], wts["lk1"], wts["lq2"], wts["lk2"]]).astype(np.float32)
        d["subg"] = wts["subg"].reshape(128).astype(np.float32)
    elif variant == "fox":
        d["w"] = qkv_cols(wts["w_in"], 4 * hg, 2, extra=True)
        d["bf"] = wts["b_f"][hg * 4:(hg + 1) * 4].astype(np.float32)
        d["tri"] = np.triu(np.ones((128, 128), np.float32))
        sel = np.zeros((128, 2, 128), np.float32)
        for r in range(6):
            sel[r, r // 3, :] = 1
        d["sel"] = sel.astype(bf16)
    elif variant == "moba":
        d["w"] = qkv_cols(wts["w_qkv"], 4 * hg, 2)
        d["pos"] = pos_l; d["invf"] = np.tile(inv_freq(16)[None, :], (128, 1))
    elif variant == "mla":
        d["w"] = w_layout(wts["w_down"]); d["pos"] = pos_l; d["invf"] = np.tile(inv_freq(32)[None, :], (128, 1))
        d["wuq"] = np.stack([np.ascontiguousarray(wts["w_uq"][:, (4 * hg + 2 * hp) * 96:(4 * hg + 2 * hp + 2) * 96].reshape(3, 128, 192).transpose(1, 0, 2)) for hp in range(2)])
        d["wukv"] = np.stack([np.ascontiguousarray(wts["w_ukv"][:, (4 * hg + 2 * hp) * 128:(4 * hg + 2 * hp + 2) * 128].reshape(2, 128, 256).transpose(1, 0, 2)) for hp in range(2)])
        d["gq"] = wts["gq"].reshape(384).astype(np.float32)
        d["gkv"] = wts["gkv"].reshape(256).astype(np.float32)
    return d

_PROGS = {}


def _prog(key, builder):
    if key not in _PROGS:
        _PROGS[key] = builder()
    return _PROGS[key]


def _run(nc, in_maps):
    res = run_bass_kernel_spmd(nc, in_maps, core_ids=list(range(8)))
    return res.results


def _cast_all(arrs):
    sizes = [a.size for a in arrs]
    total = sum(sizes)
    per = 8 * 128 * 2048
    padded = ((total + per - 1) // per) * per
    flat = np.zeros(padded, np.float32)
    off = 0
    for a in arrs:
        flat[off:off + a.size] = a.reshape(-1)
        off += a.size
    ncols = padded // (8 * 128)
    flat = flat.reshape(8, 128, ncols)
    nc = _prog(("cast", ncols), lambda: build_cast(ncols))
    res = _run(nc, [{"x": flat[c]} for c in range(8)])
    out = np.concatenate([np.asarray(res[c]["y"]).reshape(-1) for c in range(8)])
    outs = []
    off = 0
    for a in arrs:
        outs.append(out[off:off + a.size].reshape(a.shape))
        off += a.size
    return outs


def _f_weights(wo_b, lnp, win_b, convw, convb, wout_b):
    DFF = 2816
    cols = [win_b[:, k * 128:(k + 1) * 128] for k in range(22)] + [win_b[:, DFF + k * 128: DFF + (k + 1) * 128] for k in range(22)]
    winl = np.ascontiguousarray(np.stack([c.reshape(8, 128, 128).transpose(1, 0, 2) for c in cols]))
    cwl = np.stack([convw[0].reshape(22, 128).T, convw[1].reshape(22, 128).T, convw[2].reshape(22, 128).T, convb.reshape(22, 128).T], axis=-1)
    return {"wo": np.ascontiguousarray(wo_b.reshape(8, 128, 1024)), "lnp": np.ascontiguousarray(lnp.astype(np.float32)), "win": winl,
            "cw": np.ascontiguousarray(cwl.astype(np.float32)), "wout": np.ascontiguousarray(wout_b.reshape(22, 128, 1024))}


def kernel(x, positions, diff_w_qkv, diff_lambda_q1, diff_lambda_k1, diff_lambda_q2, diff_lambda_k2,
           diff_subln_g, diff_w_o, fox_w_in, fox_b_f, fox_w_o, mla_w_down, mla_q_norm_g, mla_kv_norm_g,
           mla_w_uq, mla_w_ukv, mla_w_o, moba_w_qkv, moba_w_o, ffn_w_in, ffn_conv_w, ffn_conv_b,
           ffn_w_out, ln1_g, ln1_b, ln2_g, ln2_b):
    f32 = np.float32
    x = np.asarray(x, f32); positions = np.asarray(positions, np.int32)
    Bn, S, D = x.shape
    big = [x, diff_w_qkv[0], diff_w_o[0], fox_w_in[0], fox_w_o[0], mla_w_down[0], mla_w_uq[0], mla_w_ukv[0], mla_w_o[0],
           moba_w_qkv[0], moba_w_o[0]] + [ffn_w_in[i] for i in range(4)] + [ffn_w_out[i] for i in range(4)]
    big = [np.asarray(a, f32) for a in big]
    cb = _cast_all(big)
    hb = cb[0]
    (dqkv, dwo, fwin, fwo, mdown, muq, mukv, mwo, bqkv, bwo) = cb[1:11]
    fwi = cb[11:15]; fwo_ = cb[15:19]
    h = x
    variants = ["diff", "fox", "mla", "moba"]
    wos = [dwo, fwo, mwo, bwo]
    for i in range(4):
        v = variants[i]
        if v == "diff":
            wts = {"w_qkv": dqkv, "lq1": np.asarray(diff_lambda_q1[0], f32), "lk1": np.asarray(diff_lambda_k1[0], f32),
                   "lq2": np.asarray(diff_lambda_q2[0], f32), "lk2": np.asarray(diff_lambda_k2[0], f32), "subg": np.asarray(diff_subln_g[0], f32)}
        elif v == "fox":
            wts = {"w_in": fwin, "b_f": np.asarray(fox_b_f[0], f32)}
        elif v == "mla":
            wts = {"w_down": mdown, "gq": np.asarray(mla_q_norm_g[0], f32), "gkv": np.asarray(mla_kv_norm_g[0], f32), "w_uq": muq, "w_ukv": mukv}
        else:
            wts = {"w_qkv": bqkv}
        ncA = _prog(("A", v), lambda v=v: build_A(v, NTILE=64, lambda_init=0.8 - 0.6 * math.exp(-0.3 * i)))
        insA = [prep_A(v, hb[c // 4], positions[c // 4], wts, c % 4, 64) for c in range(8)]
        resA = _run(ncA, insA)
        o_full = np.stack([np.concatenate([np.asarray(resA[b * 4 + g]["o"]) for g in range(4)], axis=1) for b in range(Bn)])
        fw = _f_weights(wos[i], np.stack([np.asarray(ln1_g[i], f32), np.asarray(ln1_b[i], f32), np.asarray(ln2_g[i], f32), np.asarray(ln2_b[i], f32)]),
                        fwi[i], np.asarray(ffn_conv_w[i], f32), np.asarray(ffn_conv_b[i], f32), fwo_[i])
        insF = []
        for c in range(8):
            b = c // 4; j = c % 4
            t0 = j * 2048 - 128
            if j == 0:
                oc = np.concatenate([np.zeros((128, D), bf16), o_full[b, 0:2048]], 0)
                hc = np.concatenate([np.zeros((128, D), f32), h[b, 0:2048]], 0)
            else:
                oc = o_full[b, t0:t0 + 2176]; hc = h[b, t0:t0 + 2176]
            d = dict(fw)
            d["oT"] = np.ascontiguousarray(oc.T.reshape(8, 128, 2176))
            d["h"] = np.ascontiguousarray(hc)
            d["flag"] = np.full((128, 1), 0.0 if j == 0 else 1.0, f32)
            insF.append(d)
        ncF = _prog(("F",), build_F)
        resF = _run(ncF, insF)
        h = np.stack([np.concatenate([np.asarray(resF[b * 4 + j]["ho"]).reshape(2048, D) for j in range(4)], 0) for b in range(Bn)]).astype(f32)
        hb = np.stack([np.concatenate([np.asarray(resF[b * 4 + j]["hbo"]).reshape(2048, D) for j in range(4)], 0) for b in range(Bn)])
    return h
```
